# Optimizing a Trainium2 kernel written in Bass

```python
import math
import jax, jax.numpy as jnp
from jax import lax
import numpy as np

D_MODEL = 1024
BATCH = 4
SEQ = 4096
DEPTH = 2

D_RNN = 1024
RNN_BLOCKS = 4
RNN_BW = D_RNN // RNN_BLOCKS
CONV_W = 4
LRU_C = 8.0
ATT_GROUPS = ((128, 1), (512, 4), (2048, 16))
N_GROUPS = len(ATT_GROUPS)
ATT_HEADS = 8
ATT_HEAD_DIM = 64
ATT_W = ATT_HEADS * ATT_HEAD_DIM
ATT_BLOCK = 128
OFF_XR = 0
OFF_YR = OFF_XR + D_RNN
OFF_Q = OFF_YR + D_RNN
OFF_K = OFF_Q + N_GROUPS * ATT_W
OFF_V = OFF_K + N_GROUPS * ATT_W
OFF_GA = OFF_V + N_GROUPS * ATT_W
OFF_GB = OFF_GA + D_MODEL
N_IN = OFF_GB + D_MODEL
N_EXPERTS = 32
TOP_K = 4
D_FF = D_MODEL
SWIGLU_ALPHA = 1.702
SWIGLU_LIMIT = 7.0
MOE_BLOCK = 128
PLE_DIM = 256
ALPHA = (2.0 * DEPTH) ** 0.25
BETA = (8.0 * DEPTH) ** -0.25
LN_EPS = 1e-5

kernel_name = 'hybrid_rglru_dilated_attn_moe_deepnorm'


def _layer_norm(x, g, b):
    xf = x.astype(jnp.float32)
    mu = jnp.mean(xf, axis=-1, keepdims=True)
    var = jnp.mean(jnp.square(xf - mu), axis=-1, keepdims=True)
    y = (xf - mu) * lax.rsqrt(var + LN_EPS)
    return (y * g.astype(jnp.float32) + b.astype(jnp.float32)).astype(x.dtype)


def _causal_conv(x, w, b):
    C = x.shape[-1]
    y = lax.conv_general_dilated(x, w[:, None, :].astype(x.dtype), window_strides=(1,),
                                 padding=((CONV_W - 1, 0),),
                                 dimension_numbers=('NWC', 'WIO', 'NWC'),
                                 feature_group_count=C)
    return y + b.astype(x.dtype)


def _rg_lru(xc, w_rg, b_rg, w_ig, b_ig, lam):
    B, S, C = xc.shape
    xf = xc.astype(jnp.float32)
    xb = xf.reshape(B, S, RNN_BLOCKS, RNN_BW)
    r = jax.nn.sigmoid(jnp.einsum('bsnc,ncd->bsnd', xb, w_rg.astype(jnp.float32)).reshape(B, S, C) + b_rg)
    i = jax.nn.sigmoid(jnp.einsum('bsnc,ncd->bsnd', xb, w_ig.astype(jnp.float32)).reshape(B, S, C) + b_ig)
    log_a = -LRU_C * r * jax.nn.softplus(-lam.astype(jnp.float32))
    a = jnp.exp(log_a)
    bx = jnp.sqrt(-jnp.expm1(2.0 * log_a)) * (i * xf)

    def combine(left, right):
        a1, b1 = left
        a2, b2 = right
        return a1 * a2, a2 * b1 + b2

    _, h = lax.associative_scan(combine, (a, bx), axis=1)
    return h.astype(xc.dtype)


def _dilated_window_attention(q, k, v, dilation, n_back):
    B, S, H, Dh = q.shape
    L = S // dilation
    nb = -(-L // ATT_BLOCK)
    Lp = nb * ATT_BLOCK

    def split(t):
        t = t.reshape(B, L, dilation, H, Dh).transpose(0, 2, 1, 3, 4).reshape(B * dilation, L, H, Dh)
        t = jnp.pad(t, ((0, 0), (0, Lp - L), (0, 0), (0, 0)))
        return t.reshape(B * dilation, nb, ATT_BLOCK, H, Dh)

    def with_prev(t):
        prev = jnp.pad(t[:, :-1], ((0, 0), (1, 0), (0, 0), (0, 0), (0, 0)))
        return jnp.concatenate([prev, t], axis=2)

    qb = split(q)
    kw = with_prev(split(k))
    vw = with_prev(split(v))
    s = jnp.einsum('bnqhd,bnkhd->bnhqk', qb, kw,
                   preferred_element_type=jnp.float32) * (Dh ** -0.5)
    qi = jnp.arange(ATT_BLOCK)[:, None]
    kj = jnp.arange(2 * ATT_BLOCK)[None, :]
    diff = ATT_BLOCK + qi - kj
    kpos = (jnp.arange(nb) * ATT_BLOCK - ATT_BLOCK)[:, None, None] + kj[None]
    valid = (diff >= 0)[None] & (diff <= n_back)[None] & (kpos >= 0)
    s = jnp.where(valid[None, :, None], s, -jnp.inf)
    lse = jax.nn.logsumexp(s, axis=-1)
    pr = jnp.exp(s - lse[..., None])
    o = jnp.einsum('bnhqk,bnkhd->bnqhd', pr.astype(vw.dtype), vw)
    o = o.reshape(B * dilation, Lp, H, Dh)[:, :L]
    o = o.reshape(B, dilation, L, H, Dh).transpose(0, 2, 1, 3, 4).reshape(B, S, H, Dh)
    lse = lse.transpose(0, 1, 3, 2).reshape(B * dilation, Lp, H)[:, :L]
    lse = lse.reshape(B, dilation, L, H).transpose(0, 2, 1, 3).reshape(B, S, H)
    return o, lse


def _mixer(u, w_in, conv_w, conv_b, w_rg, b_rg, w_ig, b_ig, lru_lambda, w_rnn_out, w_att_out, w_out):
    B, S, _ = u.shape
    z = u @ w_in
    xr, yr, q_all, k_all, v_all, ga, gb = jnp.split(
        z, [OFF_YR, OFF_Q, OFF_K, OFF_V, OFF_GA, OFF_GB], axis=-1)
    h = _rg_lru(_causal_conv(xr, conv_w, conv_b), w_rg, b_rg, w_ig, b_ig, lru_lambda)
    y_a = (jax.nn.gelu(yr) * h) @ w_rnn_out
    outs, lses = [], []
    for gi, (window, dil) in enumerate(ATT_GROUPS):
        sl = slice(gi * ATT_W, (gi + 1) * ATT_W)
        shp = (B, S, ATT_HEADS, ATT_HEAD_DIM)
        o, lse = _dilated_window_attention(q_all[..., sl].reshape(shp), k_all[..., sl].reshape(shp),
                                           v_all[..., sl].reshape(shp), dil, window // dil)
        outs.append(o)
        lses.append(lse)
    wts = jax.nn.softmax(jnp.stack(lses), axis=0)
    o = jnp.einsum('gbsh,gbshd->bshd', wts.astype(u.dtype), jnp.stack(outs))
    y_b = o.reshape(B, S, ATT_W) @ w_att_out
    merged = jax.nn.sigmoid(ga) * y_a + jax.nn.sigmoid(gb) * y_b
    return merged @ w_out


def _moe(x, w_router, b_router, w_gate, b_gate, w_up, b_up, w_down, b_down):
    B, S, D = x.shape
    T = B * S
    xt = x.reshape(T, D)
    logits = (xt @ w_router).astype(jnp.float32) + b_router.astype(jnp.float32)
    top_v, top_e = lax.top_k(logits, TOP_K)
    gates = jax.nn.softmax(top_v, axis=-1)
    flat_e = top_e.reshape(-1)
    flat_g = gates.reshape(-1)
    A = T * TOP_K
    order = jnp.argsort(flat_e)
    e_sorted = flat_e[order]
    counts = jnp.bincount(flat_e, length=N_EXPERTS)
    padded = (counts + MOE_BLOCK - 1) // MOE_BLOCK * MOE_BLOCK
    pend = jnp.cumsum(padded)
    pstart = pend - padded
    cstart = jnp.cumsum(counts) - counts
    dest = pstart[e_sorted] + (jnp.arange(A) - cstart[e_sorted])
    n_blk = -(-A // MOE_BLOCK) + N_EXPERTS
    P = n_blk * MOE_BLOCK
    buf_tok = jnp.zeros((P,), jnp.int32).at[dest].set((order // TOP_K).astype(jnp.int32))
    buf_w = jnp.zeros((P,), jnp.float32).at[dest].set(flat_g[order])
    blk_expert = jnp.minimum(jnp.searchsorted(pend, jnp.arange(n_blk) * MOE_BLOCK, side='right'),
                             N_EXPERTS - 1)
    xb = xt[buf_tok].reshape(n_blk, MOE_BLOCK, D)

    def expert_block(args):
        xe, e = args
        g = xe @ w_gate[e] + b_gate[e]
        up = xe @ w_up[e] + b_up[e]
        g = jnp.minimum(g, SWIGLU_LIMIT)
        up = jnp.clip(up, -SWIGLU_LIMIT, SWIGLU_LIMIT)
        hdn = (up + 1.0) * (g * jax.nn.sigmoid(SWIGLU_ALPHA * g))
        return hdn @ w_down[e] + b_down[e]

    yb = lax.map(expert_block, (xb, blk_expert)).reshape(P, D)
    y = jnp.zeros((T, D), jnp.float32).at[buf_tok].add(yb.astype(jnp.float32) * buf_w[:, None])
    return y.astype(x.dtype).reshape(B, S, D)


def setup_inputs(seed: int = 0) -> dict:
    key = jax.random.key(seed)
    ks = jax.random.split(key, 32)
    L = DEPTH
    nrm = lambda k, shape, scale: jax.random.normal(k, shape, jnp.float32) * scale
    col_scale = jnp.ones((N_IN,), jnp.float32).at[OFF_V:OFF_GA].set(BETA)
    u = jax.random.uniform(ks[10], (L, D_RNN), jnp.float32, 0.9, 0.999)
    s = u ** (1.0 / LRU_C)
    lru_lambda = jnp.log(s) - jnp.log1p(-s)
    return {
        'x': nrm(ks[0], (BATCH, SEQ, D_MODEL), 1.0),
        'p': nrm(ks[1], (DEPTH, BATCH, SEQ, PLE_DIM), 1.0),
        'w_in': nrm(ks[2], (L, D_MODEL, N_IN), D_MODEL ** -0.5) * col_scale,
        'conv_w': nrm(ks[3], (L, CONV_W, D_RNN), CONV_W ** -0.5),
        'conv_b': nrm(ks[4], (L, D_RNN), 0.01),
        'w_rg': nrm(ks[5], (L, RNN_BLOCKS, RNN_BW, RNN_BW), RNN_BW ** -0.5),
        'b_rg': nrm(ks[6], (L, D_RNN), 0.01),
        'w_ig': nrm(ks[7], (L, RNN_BLOCKS, RNN_BW, RNN_BW), RNN_BW ** -0.5),
        'b_ig': nrm(ks[8], (L, D_RNN), 0.01),
        'lru_lambda': lru_lambda,
        'w_rnn_out': nrm(ks[11], (L, D_RNN, D_MODEL), D_RNN ** -0.5),
        'w_att_out': nrm(ks[12], (L, ATT_W, D_MODEL), ATT_W ** -0.5),
        'w_out': nrm(ks[13], (L, D_MODEL, D_MODEL), BETA * D_MODEL ** -0.5),
        'ln1_g': 1.0 + nrm(ks[14], (L, D_MODEL), 0.02),
        'ln1_b': nrm(ks[15], (L, D_MODEL), 0.02),
        'w_router': nrm(ks[16], (L, D_MODEL, N_EXPERTS), D_MODEL ** -0.5),
        'b_router': nrm(ks[17], (L, N_EXPERTS), 0.01),
        'w_gate': nrm(ks[18], (L, N_EXPERTS, D_MODEL, D_FF), BETA * D_MODEL ** -0.5),
        'b_gate': nrm(ks[19], (L, N_EXPERTS, D_FF), 0.01),
        'w_up': nrm(ks[20], (L, N_EXPERTS, D_MODEL, D_FF), BETA * D_MODEL ** -0.5),
        'b_up': nrm(ks[21], (L, N_EXPERTS, D_FF), 0.01),
        'w_down': nrm(ks[22], (L, N_EXPERTS, D_FF, D_MODEL), BETA * D_FF ** -0.5),
        'b_down': nrm(ks[23], (L, N_EXPERTS, D_MODEL), 0.01),
        'ln2_g': 1.0 + nrm(ks[24], (L, D_MODEL), 0.02),
        'ln2_b': nrm(ks[25], (L, D_MODEL), 0.02),
        'w_ple': nrm(ks[26], (L, PLE_DIM, D_MODEL), BETA * PLE_DIM ** -0.5),
        'w_ple_gate': nrm(ks[27], (L, D_MODEL, D_MODEL), D_MODEL ** -0.5),
        'b_ple_gate': nrm(ks[28], (L, D_MODEL), 0.01),
        'ln3_g': 1.0 + nrm(ks[29], (L, D_MODEL), 0.02),
        'ln3_b': nrm(ks[30], (L, D_MODEL), 0.02),
    }


def reference(x, p, w_in, conv_w, conv_b, w_rg, b_rg, w_ig, b_ig, lru_lambda, w_rnn_out,
              w_att_out, w_out, ln1_g, ln1_b, w_router, b_router, w_gate, b_gate, w_up, b_up,
              w_down, b_down, ln2_g, ln2_b, w_ple, w_ple_gate, b_ple_gate, ln3_g, ln3_b):
    for i in range(DEPTH):
        h = _mixer(x, w_in[i], conv_w[i], conv_b[i], w_rg[i], b_rg[i], w_ig[i], b_ig[i],
                   lru_lambda[i], w_rnn_out[i], w_att_out[i], w_out[i])
        x = _layer_norm(ALPHA * x + h, ln1_g[i], ln1_b[i])
        h = _moe(x, w_router[i], b_router[i], w_gate[i], b_gate[i], w_up[i], b_up[i],
                 w_down[i], b_down[i])
        x = _layer_norm(ALPHA * x + h, ln2_g[i], ln2_b[i])
        ple = (p[i] @ w_ple[i]) * jax.nn.sigmoid(x @ w_ple_gate[i] + b_ple_gate[i])
        x = _layer_norm(ALPHA * x + ple, ln3_g[i], ln3_b[i])
    return x
```

```python
import os
import numpy as np
from contextlib import ExitStack
import concourse.bass as bass
import concourse.mybir as mybir
from concourse.bass_utils import run_bass_kernel_spmd

F32 = mybir.dt.float32
BF16 = mybir.dt.bfloat16
I32 = mybir.dt.int32
U32 = mybir.dt.uint32
AF = mybir.ActivationFunctionType
ALU = mybir.AluOpType

D = 1024
SEQ = 4096
HALF = 2048
NT = HALF // 128
ALPHA = (2.0 * 2) ** 0.25
LN_EPS = 1e-5
N_IN = 8704
NE = 32
CAP = 384


class V:
    def __init__(self, ap, name):
        self.ap = ap
        self.name = name

    def __getitem__(self, idx):
        return V(self.ap[idx], self.name)

    def rearrange(self, s, **kw):
        return V(self.ap.rearrange(s, **kw), self.name)

    def bitcast(self, dt):
        return V(self.ap.bitcast(dt), self.name)


class T(V):
    pass


def _unwrap(x):
    if isinstance(x, V):
        return x.ap
    if isinstance(x, (list, tuple)):
        return type(x)(_unwrap(y) for y in x)
    return x


class Prog:
    ENG = ('sync', 'scalar', 'vector', 'gpsimd', 'tensor')
    ROT = 30000

    def __init__(self):
        self.nc = bass.Bass("TRN2", target_bir_lowering=False)
        self.es = ExitStack()
        self.ops = {e: [] for e in self.ENG}
        self.cnt = {e: 0 for e in self.ENG}
        self.gen = {e: 0 for e in self.ENG}
        self.waited = {e: {} for e in self.ENG}
        self.bufs = {}
        self.dma_cnt = {}
        self.semkeys = []
        self.outputs = []
        self.nuniq = 0

    def dram(self, name, shape, dt, kind="Internal"):
        t = self.nc.dram_tensor(name, list(shape), dt, kind=kind)
        if kind == "ExternalOutput":
            self.outputs.append(name)
        return t.ap()

    def sbuf(self, name, shape, dt, arena=False):
        shape = list(shape)
        if not arena:
            h = self.es.enter_context(self.nc.sbuf_tensor(name, shape, dt))
            return T(h[:], name)
        esz = 4 if dt in (F32, I32, U32) else 2
        free = int(np.prod(shape[1:]))
        nbytes = (free * esz + 63) // 64 * 64
        off = self.arena_off
        assert off + nbytes <= self.arena_bytes, (name, off, nbytes, self.arena_bytes)
        self.arena_off += nbytes
        ap = self.arena_ap[0:shape[0], off // 4:(off + free * esz) // 4]
        if dt != F32:
            ap = ap.bitcast(dt)
        if len(shape) == 3:
            ap = ap.rearrange("p (a b) -> p a b", a=shape[1])
        elif len(shape) == 4:
            ap = ap.rearrange("p (a b c) -> p a b c", a=shape[1], b=shape[2])
        return T(ap, name)

    def make_arena(self, nbytes):
        h = self.es.enter_context(self.nc.sbuf_tensor("arena", [128, nbytes // 4], F32))
        self.arena_ap = h[:]
        self.arena_bytes = nbytes
        self.arena_off = 0

    def arena_reset(self, off=0):
        self.barrier()
        self.arena_off = off

    def barrier(self):
        latest = {}
        for b, st in self.bufs.items():
            evs = list(st[1].items())
            if st[0] is not None:
                evs.append(st[0])
            for k, v in evs:
                if latest.get(k, 0) < v:
                    latest[k] = v
        for eng in self.ENG:
            waits = []
            for k, v in latest.items():
                if eng == 'tensor' and k[0] == 'e' and k[1] == 'tensor':
                    continue
                if self.waited[eng].get(k, 0) < v:
                    self.waited[eng][k] = v
                    waits.append((k, v))
                    self._semkey(k)
            if waits:
                self.ops[eng].append((waits, None, (), {}, None, 0))

    def psum(self, name, shape, dt):
        h = self.es.enter_context(self.nc.psum_tensor(name, list(shape), dt))
        return T(h[:], name)

    @staticmethod
    def _names(aps):
        out = []
        for a in aps:
            if a is None:
                continue
            out.append(a if isinstance(a, str) else a.name)
        return out

    def _deps(self, eng, reads, writes):
        waits = {}

        def need(k, v):
            if eng == 'tensor' and k[0] == 'e' and k[1] == 'tensor':
                return
            if self.waited[eng].get(k, 0) >= v:
                return
            if waits.get(k, 0) < v:
                waits[k] = v

        for b in reads:
            st = self.bufs.setdefault(b, [None, {}])
            if st[0] is not None:
                need(*st[0])
        for b in writes:
            st = self.bufs.setdefault(b, [None, {}])
            if st[0] is not None:
                need(*st[0])
            for k, v in st[1].items():
                need(k, v)
        for k, v in waits.items():
            self.waited[eng][k] = v
        return list(waits.items())

    def _commit(self, ev, reads, writes):
        k, v = ev
        for b in reads:
            st = self.bufs[b]
            if st[1].get(k, 0) < v:
                st[1][k] = v
        for b in writes:
            self.bufs[b] = [ev, {}]

    def _semkey(self, key):
        if key not in self.semkeys:
            self.semkeys.append(key)
        return key

    def op(self, eng, fn, reads, writes, *args, **kw):
        reads = self._names(reads)
        writes = self._names(writes)
        waits = self._deps(eng, reads, writes)
        if self.cnt[eng] >= self.ROT:
            self.gen[eng] += 1
            self.cnt[eng] = 0
        self.cnt[eng] += 1
        key = self._semkey(('e', eng, self.gen[eng]))
        for k, _ in waits:
            self._semkey(k)
        self.ops[eng].append((waits, fn, _unwrap(args), {k: _unwrap(v) for k, v in kw.items()}, key, 1))
        self._commit((key, self.cnt[eng]), reads, writes)

    def dma(self, eng, out, in_, fn='dma_start', extra_reads=(), **kw):
        dst = out.name
        reads = self._names([in_] + list(extra_reads))
        waits = self._deps(eng, reads, [dst])
        n = self.dma_cnt.get(dst, 0) + 1
        self.dma_cnt[dst] = n
        key = self._semkey(('d', dst))
        for k, _ in waits:
            self._semkey(k)
        kw = dict(kw)
        kw['out'] = _unwrap(out)
        kw['in_'] = _unwrap(in_)
        kw = {k: _unwrap(v) for k, v in kw.items()}
        self.ops[eng].append((waits, fn, (), kw, key, 16))
        self._commit((key, 16 * n), reads, [dst])

    def finish(self):
        waits = {}
        for b, st in self.bufs.items():
            evs = []
            if st[0] is not None:
                evs.append(st[0])
            evs += list(st[1].items())
            for k, v in evs:
                if self.waited['sync'].get(k, 0) < v and waits.get(k, 0) < v:
                    waits[k] = v
        for k in waits:
            self._semkey(k)
        self.ops['sync'].append((list(waits.items()), None, (), {}, None, 0))

    def build(self):
        nc = self.nc
        sems = {}
        for i, key in enumerate(self.semkeys):
            nm = "s%d_" % i + "_".join(str(x) for x in key)
            sems[key] = self.es.enter_context(nc.semaphore(nm[:40]))
        with nc.Block() as block:
            for eng in self.ENG:
                def body(e, eng=eng):
                    for (waits, fn, args, kw, key, inc) in self.ops[eng]:
                        for (k, v) in waits:
                            e.wait_ge(sems[k], v)
                        if fn is None:
                            continue
                        ins = getattr(e, fn)(*args, **kw)
                        ins.then_inc(sems[key], inc)
                getattr(block, eng)(body)
        self.es.close()
        return nc


def load_bcast(P, name, dram_vec, n):
    t = P.sbuf(name, [128, n], F32)
    P.dma('sync', t[:], dram_vec.partition_broadcast(128))
    return t


def make_ident(P):
    io = P.sbuf("id_iota", [128, 128], F32)
    pid = P.sbuf("id_pid", [128, 1], F32)
    ident = P.sbuf("ident", [128, 128], BF16)
    P.op('gpsimd', 'iota', [], [io], io[:], [[1, 128]], base=0, channel_multiplier=0,
         allow_small_or_imprecise_dtypes=True)
    P.op('gpsimd', 'iota', [], [pid], pid[:], [[1, 1]], base=0, channel_multiplier=1,
         allow_small_or_imprecise_dtypes=True)
    P.op('vector', 'tensor_scalar', [io, pid], [ident], out=ident[:], in0=io[:], scalar1=pid[:, 0:1],
         scalar2=None, op0=ALU.is_equal)
    return ident


def layer_norm(P, y, out, gam, bet, tag, scr):
    stats, mv, rstd = scr
    P.op('vector', 'bn_stats', [y], [stats], out=stats[:, 0, :], in_=y[:, 0:512])
    P.op('vector', 'bn_stats', [y], [stats], out=stats[:, 1, :], in_=y[:, 512:1024])
    P.op('vector', 'bn_aggr', [stats], [mv], out=mv[:], in_=stats[:].rearrange("p a b -> p (a b)"))
    P.op('vector', 'tensor_scalar_add', [mv], [rstd], out=rstd[:], in0=mv[:, 1:2], scalar1=LN_EPS)
    P.op('scalar', 'sqrt', [rstd], [rstd], out=rstd[:], in_=rstd[:])
    P.op('vector', 'reciprocal', [rstd], [rstd], out=rstd[:], in_=rstd[:])
    P.op('vector', 'tensor_scalar', [y, mv, rstd], [out], out=out, in0=y, scalar1=mv[:, 0:1],
         scalar2=rstd[:, 0:1], op0=ALU.subtract, op1=ALU.mult)
    P.op('gpsimd', 'tensor_mul', [out, gam], [out], out=out, in0=out, in1=gam[:])
    P.op('gpsimd', 'tensor_add', [out, bet], [out], out=out, in0=out, in1=bet[:])


def ln_scratch(P, tag):
    return (P.sbuf("ln_stats" + tag, [128, 2, 6], F32), P.sbuf("ln_mv" + tag, [128, 2], F32),
            P.sbuf("ln_rstd" + tag, [128, 1], F32))


def load_w_bf16(P, name, w_dram, kchunks, ncols):
    t = P.sbuf(name, [128, kchunks, ncols], BF16)
    src = w_dram.rearrange("(k p) n -> p k n", p=128)
    for k in range(kchunks):
        P.dma('gpsimd', t[:, k, :], src[:, k, :])
    return t


def transpose_to(P, ident, src_bf, dst, ps, nchunks, evac_eng='scalar'):
    for k in range(nchunks):
        P.op('tensor', 'transpose', [src_bf, ident], [ps], out=ps[:, k * 128:(k + 1) * 128],
             in_=src_bf[:, k * 128:(k + 1) * 128], identity=ident[:])
    if evac_eng == 'scalar':
        P.op('scalar', 'copy', [ps], [dst], out=dst[:].rearrange("p k t -> p (k t)"), in_=ps[:, 0:nchunks * 128])
    else:
        P.op(evac_eng, 'tensor_copy', [ps], [dst], out=dst[:].rearrange("p k t -> p (k t)"),
             in_=ps[:, 0:nchunks * 128])


def build_ple():
    P = Prog()
    x2 = P.dram("x2", [HALF, D], F32, "ExternalInput")
    pin = P.dram("p", [HALF, 256], F32, "ExternalInput")
    w_ple = P.dram("w_ple", [256, D], F32, "ExternalInput")
    w_pg = P.dram("w_ple_gate", [D, D], F32, "ExternalInput")
    b_pg = P.dram("b_ple_gate", [D], F32, "ExternalInput")
    g3 = P.dram("ln3_g", [D], F32, "ExternalInput")
    b3 = P.dram("ln3_b", [D], F32, "ExternalInput")
    x3 = P.dram("x3", [HALF, D], F32, "ExternalOutput")
    emit_ple(P, x2, pin, w_ple, w_pg, b_pg, g3, b3, x3)
    P.finish()
    return P.build()


def emit_ple(P, x2, pin, w_ple, w_pg, b_pg, g3, b3, x3):
    ident = make_ident(P)
    gam = load_bcast(P, "ple_gam", g3, D)
    bet = load_bcast(P, "ple_bet", b3, D)
    bias = load_bcast(P, "ple_bias", b_pg, D)
    wg = load_w_bf16(P, "ple_wg", w_pg, 8, D)
    wp = load_w_bf16(P, "ple_wp", w_ple, 2, D)
    scr = ln_scratch(P, "ple")
    NB = 2
    xt = [P.sbuf("ple_x%d" % i, [128, D], F32) for i in range(NB)]
    pt = [P.sbuf("ple_p%d" % i, [128, 256], F32) for i in range(NB)]
    xb = [P.sbuf("ple_xb%d" % i, [128, D], BF16) for i in range(NB)]
    pb = [P.sbuf("ple_pb%d" % i, [128, 256], BF16) for i in range(NB)]
    xT = [P.sbuf("ple_xT%d" % i, [128, 8, 128], BF16) for i in range(NB)]
    pT = [P.sbuf("ple_pT%d" % i, [128, 2, 128], BF16) for i in range(NB)]
    gt = [P.sbuf("ple_g%d" % i, [128, D], F32) for i in range(NB)]
    yt = [P.sbuf("ple_y%d" % i, [128, D], F32) for i in range(NB)]
    ot = [P.sbuf("ple_o%d" % i, [128, D], F32) for i in range(NB)]
    ps_t = [P.psum("ple_pst%d" % i, [128, 1024], BF16) for i in range(2)]
    ps_g = [P.psum("ple_psg%d" % i, [128, 512], F32) for i in range(2)]
    ps_p = [P.psum("ple_psp%d" % i, [128, 512], F32) for i in range(2)]
    for t in range(NT):
        i = t % NB
        rows = slice(t * 128, (t + 1) * 128)
        P.dma('sync', xt[i][:], x2[rows, :])
        P.dma('sync', pt[i][:], pin[rows, :])
        P.op('scalar', 'copy', [xt[i]], [xb[i]], out=xb[i][:], in_=xt[i][:])
        P.op('vector', 'tensor_copy', [pt[i]], [pb[i]], out=pb[i][:], in_=pt[i][:])
        transpose_to(P, ident, xb[i], xT[i], ps_t[0], 8, 'scalar')
        transpose_to(P, ident, pb[i], pT[i], ps_t[1], 2, 'vector')
        for n in range(2):
            cs = slice(n * 512, (n + 1) * 512)
            for k in range(8):
                P.op('tensor', 'matmul', [xT[i], wg], [ps_g[n]], ps_g[n][:], lhsT=xT[i][:, k, :], rhs=wg[:, k, cs],
                     start=(k == 0), stop=(k == 7))
            P.op('vector', 'tensor_tensor', [ps_g[n], bias], [gt[i]], out=gt[i][:, cs], in0=ps_g[n][:],
                 in1=bias[:, cs], op=ALU.add)
            P.op('scalar', 'activation', [gt[i]], [gt[i]], out=gt[i][:, cs], in_=gt[i][:, cs], func=AF.Sigmoid)
            for k in range(2):
                P.op('tensor', 'matmul', [pT[i], wp], [ps_p[n]], ps_p[n][:], lhsT=pT[i][:, k, :], rhs=wp[:, k, cs],
                     start=(k == 0), stop=(k == 1))
            P.op('vector', 'tensor_tensor', [ps_p[n], gt[i]], [yt[i]], out=yt[i][:, cs], in0=ps_p[n][:],
                 in1=gt[i][:, cs], op=ALU.mult)
        P.op('vector', 'scalar_tensor_tensor', [xt[i], yt[i]], [yt[i]], out=yt[i][:], in0=xt[i][:], scalar=ALPHA,
             in1=yt[i][:], op0=ALU.mult, op1=ALU.add)
        layer_norm(P, yt[i][:], ot[i][:], gam, bet, "ple", scr)
        P.dma('sync', x3[rows, :], ot[i][:])


_CACHE = {}


def _get(name, fn):
    if name not in _CACHE:
        _CACHE[name] = fn()
    return _CACHE[name]


def run_ple(x2_full, p_l, w):
    nc = _get("ple", build_ple)
    in_maps = []
    for c in range(8):
        b, h = c // 2, c % 2
        sl = slice(h * HALF, (h + 1) * HALF)
        in_maps.append({"x2": np.ascontiguousarray(x2_full[b, sl]), "p": np.ascontiguousarray(p_l[b, sl]),
                        "w_ple": w["w_ple"], "w_ple_gate": w["w_ple_gate"], "b_ple_gate": w["b_ple_gate"],
                        "ln3_g": w["ln3_g"], "ln3_b": w["ln3_b"]})
    res = run_bass_kernel_spmd(nc, in_maps, core_ids=list(range(8)))
    out = np.empty((4, SEQ, D), np.float32)
    for c in range(8):
        b, h = c // 2, c % 2
        out[b, h * HALF:(h + 1) * HALF] = res.results[c]["x3"]
    return out


OFF_XR, OFF_YR, OFF_Q, OFF_K, OFF_V, OFF_GA, OFF_GB = 0, 1024, 2048, 3584, 5120, 6656, 7680
GROUP_DIL = (1, 4, 16)
GELU_C = 1.5957691216057308


def build_mixer():
    P = Prog()
    xh = P.dram("xh", [2 * HALF, D], F32, "ExternalInput")
    w_in = P.dram("w_in", [D, N_IN], F32, "ExternalInput")
    chp = P.dram("chp", [128, 8, 8], F32, "ExternalInput")
    w_rg = P.dram("w_rg", [D, 256], F32, "ExternalInput")
    w_ig = P.dram("w_ig", [D, 256], F32, "ExternalInput")
    w_ro = P.dram("w_rnn_out", [D, D], F32, "ExternalInput")
    w_ao = P.dram("w_att_out", [512, D], F32, "ExternalInput")
    w_o = P.dram("w_out", [D, D], F32, "ExternalInput")
    g1 = P.dram("ln1_g", [D], F32, "ExternalInput")
    b1 = P.dram("ln1_b", [D], F32, "ExternalInput")
    mask2 = P.dram("mask2", [128, 2, 256], F32, "ExternalInput")
    hmask = P.dram("hmask", [128, 2, 128], F32, "ExternalInput")
    flag = P.dram("flag", [128, 1], F32, "ExternalInput")
    x1 = P.dram("x1", [HALF, D], F32, "ExternalOutput")
    emit_mixer(P, xh, w_in, chp, w_rg, w_ig, w_ro, w_ao, w_o, g1, b1, mask2, hmask, flag, x1)
    P.finish()
    return P.build()


def emit_mixer(P, xh, w_in, chp_d, w_rg, w_ig, w_ro, w_ao, w_o, g1, b1, mask2_d, hmask_d, flag_d, x1):
    ident = make_ident(P)
    PS = [P.psum("ps%d" % i, [128, 512], F32) for i in range(8)]
    uTo = P.sbuf("uTo", [128, 8, HALF], BF16)
    actT = P.sbuf("actT", [128, 8, HALF], BF16)
    oT = P.sbuf("oT", [128, 4, HALF], BF16)
    chp = P.sbuf("chp_sb", [128, 8, 8], F32)
    P.dma('sync', chp[:], chp_d)
    flag = P.sbuf("flag_sb", [128, 1], F32)
    P.dma('sync', flag[:], flag_d)
    gam = load_bcast(P, "mx_gam", g1, D)
    bet = load_bcast(P, "mx_bet", b1, D)
    P.make_arena(118 * 1024)
    uTh = P.sbuf("uTh", [128, 8, HALF], BF16, arena=True)
    arena_base = P.arena_off
    w_in_v = w_in.rearrange("(k p) n -> p k n", p=128)

    def uT_tile(j):
        return (uTh if j < 4 else uTo), (j % 4) * 512

    wpool = {}

    def wchunk(col0, ncols=128):
        lst = wpool['bufs']
        i = wpool['i']
        wpool['i'] = (i + 1) % len(lst)
        t, stg = lst[i]
        P.dma('sync', stg[:, :, 0:ncols], w_in_v[:, :, col0:col0 + ncols])
        P.op('gpsimd', 'tensor_copy', [stg], [t], out=t[:, :, 0:ncols], in_=stg[:, :, 0:ncols])
        return t

    xa = [P.sbuf("a_x%d" % i, [128, D], F32, arena=True) for i in range(2)]
    xab = [P.sbuf("a_xb%d" % i, [128, D], BF16, arena=True) for i in range(2)]
    for t in range(32):
        i = t % 2
        P.dma('sync', xa[i][:], xh[t * 128:(t + 1) * 128, :])
        P.op('vector', 'tensor_copy', [xa[i]], [xab[i]], out=xab[i][:], in_=xa[i][:])
        ps = PS[t % 2]
        psb = ps[:].bitcast(BF16)
        for k in range(8):
            P.op('tensor', 'transpose', [xab[i], ident], [ps], out=psb[:, k * 128:(k + 1) * 128],
                 in_=xab[i][:, k * 128:(k + 1) * 128], identity=ident[:])
        dst = uTh if t < 16 else uTo
        tt = t % 16
        P.op('scalar', 'copy', [ps], [dst], out=dst[:, :, tt * 128:(tt + 1) * 128],
             in_=psb.rearrange("p (k t) -> p k t", k=8))

    if os.environ.get('MIXER_STOP') == 'A':
        return
    P.arena_reset(arena_base)
    wpool['bufs'] = [(P.sbuf("b_w%d" % i, [128, 8, 128], BF16, arena=True),
                      P.sbuf("b_ws%d" % i, [128, 8, 128], F32, arena=True)) for i in range(4)]
    wpool['i'] = 0
    wrg = P.sbuf("b_wrg", [128, 8, 256], BF16, arena=True)
    wig = P.sbuf("b_wig", [128, 8, 256], BF16, arena=True)
    for k in range(8):
        P.dma('gpsimd', wrg[:, k, :], w_rg[k * 128:(k + 1) * 128, :])
        P.dma('gpsimd', wig[:, k, :], w_ig[k * 128:(k + 1) * 128, :])
    cl = P.sbuf("b_cl", [128, 8], F32, arena=True)
    P.op('scalar', 'activation', [chp], [cl], out=cl[:], in_=chp[:, :, 7], func=AF.Exp, scale=-1.0)
    P.op('scalar', 'activation', [cl], [cl], out=cl[:], in_=cl[:], func=AF.Ln, bias=1.0)
    P.op('vector', 'tensor_scalar_mul', [cl], [cl], out=cl[:], in0=cl[:], scalar1=-8.0)

    def bt(name, n=2, w=512, dt=F32):
        return [P.sbuf("b_%s%d" % (name, i), [128, w], dt, arena=True) for i in range(n)]
    xr = bt("xr", 4, 515)
    xc = bt("xc", 2)
    xcb = P.sbuf("b_xcb", [128, 2, 512], BF16, arena=True)
    rr = bt("r", 2)
    ii = bt("i", 2)
    aa = bt("a", 2)
    ss = bt("s", 2)
    bx = bt("bx", 2)
    hh = bt("h", 4)
    y2 = bt("y2", 2)
    sg = bt("sg", 2)
    bstop = int(os.environ.get('MIXER_BSTOP', '1000'))
    bpart = int(os.environ.get('MIXER_BPART', '4'))
    bcount = 0
    for n in range(4):
        for j in range(8):
            bcount += 1
            if bcount > bstop:
                continue
            uT, c0 = uT_tile(j)
            own = j >= 4
            for lc in range(2):
                c = 2 * n + lc
                wt = wchunk(OFF_XR + c * 128)
                ps = PS[lc]
                for k in range(8):
                    P.op('tensor', 'matmul', [wt, uT], [ps], ps[:], lhsT=wt[:, k, :], rhs=uT[:, k, c0:c0 + 512],
                         start=(k == 0), stop=(k == 7))
                xcur = xr[lc + 2 * (j % 2)]
                xprev = xr[lc + 2 * ((j + 1) % 2)]
                if j == 0:
                    P.op('vector', 'memset', [], [xcur], xcur[:, 0:3], 0.0)
                else:
                    P.op('vector', 'tensor_copy', [xprev], [xcur], out=xcur[:, 0:3], in_=xprev[:, 512:515])
                P.op('scalar', 'copy', [ps], [xcur], out=xcur[:, 3:515], in_=ps[:])
                P.op('vector', 'tensor_scalar', [xcur, chp], [xc[lc]], out=xc[lc][:], in0=xcur[:, 0:512],
                     scalar1=chp[:, c, 0:1], scalar2=chp[:, c, 4:5], op0=ALU.mult, op1=ALU.add)
                for w_i in range(1, 4):
                    P.op('vector', 'scalar_tensor_tensor', [xcur, chp, xc[lc]], [xc[lc]], out=xc[lc][:],
                         in0=xcur[:, w_i:w_i + 512], scalar=chp[:, c, w_i:w_i + 1], in1=xc[lc][:],
                         op0=ALU.mult, op1=ALU.add)
                P.op('scalar', 'copy', [xc[lc]], [xcb], out=xcb[:, lc, :], in_=xc[lc][:])
            for lc in range(2):
                if bpart < 2:
                    continue
                c = 2 * n + lc
                psr, psi = PS[2 + lc], PS[4 + lc]
                for kc in range(2):
                    P.op('tensor', 'matmul', [wrg, xcb], [psr], psr[:], lhsT=wrg[:, 2 * n + kc, lc * 128:(lc + 1) * 128],
                         rhs=xcb[:, kc, :], start=(kc == 0), stop=(kc == 1))
                for kc in range(2):
                    P.op('tensor', 'matmul', [wig, xcb], [psi], psi[:], lhsT=wig[:, 2 * n + kc, lc * 128:(lc + 1) * 128],
                         rhs=xcb[:, kc, :], start=(kc == 0), stop=(kc == 1))
                P.op('scalar', 'activation', [psr, chp], [rr[lc]], out=rr[lc][:], in_=psr[:], func=AF.Sigmoid,
                     bias=chp[:, c, 5:6])
                P.op('scalar', 'activation', [psi, chp], [ii[lc]], out=ii[lc][:], in_=psi[:], func=AF.Sigmoid,
                     bias=chp[:, c, 6:7])
                if bpart < 3:
                    continue
                P.op('scalar', 'activation', [rr[lc], cl], [aa[lc]], out=aa[lc][:], in_=rr[lc][:], func=AF.Exp,
                     scale=cl[:, c:c + 1])
                P.op('vector', 'tensor_tensor', [aa[lc]], [ss[lc]], out=ss[lc][:], in0=aa[lc][:], in1=aa[lc][:],
                     op=ALU.mult)
                P.op('scalar', 'activation', [ss[lc]], [ss[lc]], out=ss[lc][:], in_=ss[lc][:], func=AF.Sqrt,
                     scale=-1.0, bias=1.0)
                P.op('vector', 'tensor_tensor', [ii[lc], xc[lc]], [bx[lc]], out=bx[lc][:], in0=ii[lc][:], in1=xc[lc][:], op=ALU.mult)
                if own:
                    P.op('vector', 'tensor_tensor', [ss[lc], bx[lc]], [bx[lc]], out=bx[lc][:], in0=ss[lc][:],
                         in1=bx[lc][:], op=ALU.mult)
                else:
                    P.op('vector', 'scalar_tensor_tensor', [ss[lc], flag, bx[lc]], [bx[lc]], out=bx[lc][:],
                         in0=ss[lc][:], scalar=flag[:, 0:1], in1=bx[lc][:], op0=ALU.mult, op1=ALU.mult)
                hcur = hh[lc + 2 * (j % 2)]
                hprev = hh[lc + 2 * ((j + 1) % 2)]
                init = 0.0 if j == 0 else hprev[:, 511:512]
                P.op('vector', 'tensor_tensor_scan', [aa[lc], bx[lc]] + ([] if j == 0 else [hprev]), [hcur],
                     out=hcur[:], data0=aa[lc][:], data1=bx[lc][:], initial=init, op0=ALU.mult, op1=ALU.add)
                if own and bpart >= 4:
                    wt = wchunk(OFF_YR + c * 128)
                    psy = PS[6 + lc]
                    for k in range(8):
                        P.op('tensor', 'matmul', [wt, uT], [psy], psy[:], lhsT=wt[:, k, :], rhs=uT[:, k, c0:c0 + 512],
                             start=(k == 0), stop=(k == 7))
                    P.op('scalar', 'activation', [psy], [y2[lc]], out=y2[lc][:], in_=psy[:], func=AF.Square)
                    P.op('vector', 'tensor_scalar', [y2[lc]], [y2[lc]], out=y2[lc][:], in0=y2[lc][:], scalar1=0.044715,
                         scalar2=1.0, op0=ALU.mult, op1=ALU.add)
                    P.op('vector', 'tensor_tensor', [y2[lc], psy], [y2[lc]], out=y2[lc][:], in0=psy[:], in1=y2[lc][:],
                         op=ALU.mult)
                    P.op('scalar', 'activation', [y2[lc]], [sg[lc]], out=sg[lc][:], in_=y2[lc][:], func=AF.Sigmoid,
                         scale=GELU_C)
                    P.op('vector', 'tensor_tensor', [sg[lc], psy], [sg[lc]], out=sg[lc][:], in0=psy[:], in1=sg[lc][:],
                         op=ALU.mult)
                    P.op('vector', 'tensor_tensor', [sg[lc], hcur], [actT], out=actT[:, c, (j - 4) * 512:(j - 3) * 512],
                         in0=sg[lc][:], in1=hcur[:], op=ALU.mult)

    if os.environ.get('MIXER_STOP') == 'B':
        return
    P.arena_reset(arena_base)
    wpool['bufs'] = [(P.sbuf("c_w%d" % i, [128, 8, 128], BF16, arena=True),
                      P.sbuf("c_ws%d" % i, [128, 8, 128], F32, arena=True)) for i in range(4)]
    wpool['i'] = 0
    m2f = P.sbuf("c_m2f", [128, 2, 256], F32, arena=True)
    hmf = P.sbuf("c_hmf", [128, 2, 128], F32, arena=True)
    m2 = P.sbuf("c_m2", [128, 2, 256], BF16, arena=True)
    hm = P.sbuf("c_hm", [128, 2, 128], BF16, arena=True)
    P.dma('sync', m2f[:], mask2_d)
    P.dma('sync', hmf[:], hmask_d)
    P.op('vector', 'tensor_copy', [m2f], [m2], out=m2[:], in_=m2f[:])
    P.op('vector', 'tensor_copy', [hmf], [hm], out=hm[:], in_=hmf[:])
    ones = P.sbuf("c_ones", [128, 64], BF16, arena=True)
    P.op('vector', 'memset', [], [ones], ones[:], 1.0)
    qT = P.sbuf("c_qT", [128, HALF], BF16, arena=True)
    kT = P.sbuf("c_kT", [128, 2 * HALF], BF16, arena=True)
    vT = P.sbuf("c_vT", [128, 2 * HALF], BF16, arena=True)
    Vt = P.sbuf("c_V", [128, 32, 128], BF16, arena=True)
    PT = [P.sbuf("c_PT%d" % i, [128, 2, 256], BF16, arena=True) for i in range(3)]
    num = P.sbuf("c_num", [128, HALF], F32, arena=True)
    den = P.sbuf("c_den", [128, HALF], F32, arena=True)
    for sp in range(4):
        for g in range(3):
            d = GROUP_DIL[g]
            nres = d
            nbr = 16 // d
            colq = OFF_Q + g * 512 + sp * 128
            colk = OFF_K + g * 512 + sp * 128
            colv = OFF_V + g * 512 + sp * 128
            wt = wchunk(colq)
            for j in range(4):
                ps = PS[j % 2]
                for k in range(8):
                    P.op('tensor', 'matmul', [wt, uTo], [ps], ps[:], lhsT=wt[:, k, :], rhs=uTo[:, k, j * 512:(j + 1) * 512],
                         start=(k == 0), stop=(k == 7))
                P.op('scalar', 'activation', [ps], [qT],
                     out=qT[:].rearrange("p (r n) -> p r n", r=d)[:, :, j * 512 // d:(j + 1) * 512 // d],
                     in_=ps[:].rearrange("p (n r) -> p r n", r=d), func=AF.Copy, scale=0.125)
            wt = wchunk(colk)
            j0 = 0 if g == 2 else 3
            for j in range(j0, 8):
                uT, c0 = uT_tile(j)
                ps = PS[2 + j % 2]
                for k in range(8):
                    P.op('tensor', 'matmul', [wt, uT], [ps], ps[:], lhsT=wt[:, k, :], rhs=uT[:, k, c0:c0 + 512],
                         start=(k == 0), stop=(k == 7))
                P.op('vector', 'tensor_copy', [ps], [kT],
                     out=kT[:].rearrange("p (r n) -> p r n", r=d)[:, :, j * 512 // d:(j + 1) * 512 // d],
                     in_=ps[:].rearrange("p (n r) -> p r n", r=d))
            wt = wchunk(colv)
            for j in range(j0, 8):
                uT, c0 = uT_tile(j)
                ps = PS[4 + j % 2]
                for k in range(8):
                    P.op('tensor', 'matmul', [wt, uT], [ps], ps[:], lhsT=wt[:, k, :], rhs=uT[:, k, c0:c0 + 512],
                         start=(k == 0), stop=(k == 7))
                P.op('scalar', 'copy', [ps], [vT],
                     out=vT[:].rearrange("p (r n) -> p r n", r=d)[:, :, j * 512 // d:(j + 1) * 512 // d],
                     in_=ps[:].rearrange("p (n r) -> p r n", r=d))
            vv = vT[:].rearrange("p (r n) -> p r n", r=d)
            nblk = nres * (nbr + 1)
            blocks = [(r, nb) for r in range(nres) for nb in range(-1, nbr)]
            for b0 in range(0, nblk, 8):
                ps = PS[6 + (b0 // 8) % 2]
                psb = ps[:].bitcast(BF16)
                nbi = min(8, nblk - b0)
                for bi in range(nbi):
                    r, nb = blocks[b0 + bi]
                    n0 = HALF // d + nb * 128
                    P.op('tensor', 'transpose', [vT, ident], [ps], out=psb[:, bi * 128:(bi + 1) * 128],
                         in_=vv[:, r, n0:n0 + 128], identity=ident[:])
                P.op('vector', 'tensor_copy', [ps], [Vt], out=Vt[:, b0:b0 + nbi, :],
                     in_=psb[:, 0:nbi * 128].rearrange("p (b c) -> p b c", b=nbi))
            qv = qT[:].rearrange("p (r n) -> p r n", r=d)
            kv = kT[:].rearrange("p (r n) -> p r n", r=d)
            nv = num[:].rearrange("p (n r) -> p r n", r=d)
            dv = den[:].rearrange("p (n r) -> p r n", r=d)
            koff = HALF // d
            pti = 0
            qb_count = 0
            for r in range(nres):
                prevPT = None
                for nb in range(-1, nbr):
                    pt = PT[pti % 3]
                    pti += 1
                    pssh = (PS[pti % 2], PS[6 + pti % 2])
                    kcols = slice(koff + nb * 128, koff + (nb + 1) * 128)
                    if nb < 0:
                        qlo, qhi, plo = 0, 128, 128
                    elif nb == nbr - 1:
                        qlo, qhi, plo = nb * 128, (nb + 1) * 128, 0
                    else:
                        qlo, qhi, plo = nb * 128, (nb + 2) * 128, 0
                    nq = qhi - qlo
                    for h2 in range(2):
                        rows = slice(h2 * 64, (h2 + 1) * 64)
                        P.op('tensor', 'matmul', [kT, qT], [pssh[h2]], pssh[h2][:, plo:plo + nq], lhsT=kv[rows, r, kcols],
                             rhs=qv[rows, r, qlo:qhi], start=True, stop=True)
                    for h2 in range(2):
                        P.op('scalar', 'activation', [pssh[h2]], [pt], out=pt[:, h2, plo:plo + nq],
                             in_=pssh[h2][:, plo:plo + nq], func=AF.Exp)
                    if nb < 0:
                        P.op('vector', 'tensor_tensor', [pt, hm], [pt], out=pt[:, :, 128:256], in0=pt[:, :, 128:256],
                             in1=hm[:], op=ALU.mult)
                    else:
                        P.op('vector', 'tensor_tensor', [pt, m2], [pt], out=pt[:, :, 0:nq], in0=pt[:, :, 0:nq],
                             in1=m2[:, :, 0:nq], op=ALU.mult)
                    if nb >= 0:
                        slot = qb_count % 4
                        bank = (qb_count // 4) % 2
                        psn, psd = PS[2 + bank], PS[4 + bank]
                        ocols = slice(slot * 128, (slot + 1) * 128)
                        bprev = r * (nbr + 1) + nb
                        bcur = bprev + 1
                        for h2 in range(2):
                            rows = slice(h2 * 64, (h2 + 1) * 64)
                            vc = slice(h2 * 64, (h2 + 1) * 64)
                            P.op('tensor', 'matmul', [Vt, prevPT], [psn], psn[rows, ocols], lhsT=Vt[:, bprev, vc],
                                 rhs=prevPT[:, h2, 128:256], start=True, stop=False)
                            P.op('tensor', 'matmul', [Vt, pt], [psn], psn[rows, ocols], lhsT=Vt[:, bcur, vc],
                                 rhs=pt[:, h2, 0:128], start=False, stop=True)
                            P.op('tensor', 'matmul', [ones, prevPT], [psd], psd[rows, ocols], lhsT=ones[:],
                                 rhs=prevPT[:, h2, 128:256], start=True, stop=False)
                            P.op('tensor', 'matmul', [ones, pt], [psd], psd[rows, ocols], lhsT=ones[:],
                                 rhs=pt[:, h2, 0:128], start=False, stop=True)
                        qb_count += 1
                        if qb_count % 4 == 0:
                            if g == 0:
                                q0 = (qb_count - 4) * 128
                                outs = [nv[:, 0, q0:q0 + 512], dv[:, 0, q0:q0 + 512]]
                                ins = [psn[:], psd[:]]
                            elif g == 1:
                                outs = [nv[:, r, :], dv[:, r, :]]
                                ins = [psn[:], psd[:]]
                            else:
                                outs = [nv[:, r - 3:r + 1, :], dv[:, r - 3:r + 1, :]]
                                ins = [psn[:].rearrange("p (b q) -> p b q", b=4), psd[:].rearrange("p (b q) -> p b q", b=4)]
                            for (o_, i_, acc, psx, eng) in ((outs[0], ins[0], num, psn, 'vector'),
                                                            (outs[1], ins[1], den, psd, 'gpsimd')):
                                if eng == 'gpsimd':
                                    eng = 'vector'
                                if g == 0:
                                    P.op(eng, 'tensor_copy', [psx], [acc], out=o_, in_=i_)
                                else:
                                    P.op(eng, 'tensor_tensor', [psx, acc], [acc], out=o_, in0=i_, in1=o_, op=ALU.add)
                    prevPT = pt
        P.op('vector', 'reciprocal', [den], [den], out=den[:], in_=den[:])
        P.op('vector', 'tensor_tensor', [num, den], [oT], out=oT[:, sp, :], in0=num[:], in1=den[:], op=ALU.mult)

    if os.environ.get('MIXER_STOP') == 'C':
        return
    P.arena_reset(0)
    wpool['bufs'] = [(P.sbuf("d_w%d" % i, [128, 8, 128], BF16, arena=True),
                      P.sbuf("d_ws%d" % i, [128, 8, 128], F32, arena=True)) for i in range(4)]
    wpool['i'] = 0
    wro = P.sbuf("d_wro", [128, 8, D], BF16, arena=True)
    wao = P.sbuf("d_wao", [128, 4, D], BF16, arena=True)
    wo = P.sbuf("d_wo", [128, 8, D], BF16, arena=True)
    for k in range(8):
        P.dma('gpsimd', wro[:, k, :], w_ro[k * 128:(k + 1) * 128, :])
        P.dma('gpsimd', wo[:, k, :], w_o[k * 128:(k + 1) * 128, :])
    for k in range(4):
        P.dma('gpsimd', wao[:, k, :], w_ao[k * 128:(k + 1) * 128, :])
    mT = P.sbuf("d_mT", [128, 8, 512], BF16, arena=True)
    sga = [P.sbuf("d_sga%d" % i, [128, 512], F32, arena=True) for i in range(2)]
    sgb = [P.sbuf("d_sgb%d" % i, [128, 512], F32, arena=True) for i in range(2)]
    xt = [P.sbuf("d_x%d" % i, [128, D], F32, arena=True) for i in range(2)]
    yt = [P.sbuf("d_y%d" % i, [128, D], F32, arena=True) for i in range(2)]
    ot = [P.sbuf("d_o%d" % i, [128, D], F32, arena=True) for i in range(2)]
    scr = ln_scratch(P, "mx")
    for j in range(4):
        cs = slice(j * 512, (j + 1) * 512)
        for c in range(8):
            i = c % 2
            wga = wchunk(OFF_GA + c * 128)
            wgb = wchunk(OFF_GB + c * 128)
            pa, pb, pya, pyb = PS[0 + i], PS[2 + i], PS[4 + i], PS[6 + i]
            for k in range(8):
                P.op('tensor', 'matmul', [wga, uTo], [pa], pa[:], lhsT=wga[:, k, :], rhs=uTo[:, k, cs],
                     start=(k == 0), stop=(k == 7))
            for k in range(8):
                P.op('tensor', 'matmul', [wgb, uTo], [pb], pb[:], lhsT=wgb[:, k, :], rhs=uTo[:, k, cs],
                     start=(k == 0), stop=(k == 7))
            for k in range(8):
                P.op('tensor', 'matmul', [wro, actT], [pya], pya[:], lhsT=wro[:, k, c * 128:(c + 1) * 128],
                     rhs=actT[:, k, cs], start=(k == 0), stop=(k == 7))
            for k in range(4):
                P.op('tensor', 'matmul', [wao, oT], [pyb], pyb[:], lhsT=wao[:, k, c * 128:(c + 1) * 128],
                     rhs=oT[:, k, cs], start=(k == 0), stop=(k == 3))
            P.op('scalar', 'activation', [pa], [sga[i]], out=sga[i][:], in_=pa[:], func=AF.Sigmoid)
            P.op('scalar', 'activation', [pb], [sgb[i]], out=sgb[i][:], in_=pb[:], func=AF.Sigmoid)
            P.op('vector', 'tensor_tensor', [sga[i], pya], [sga[i]], out=sga[i][:], in0=pya[:], in1=sga[i][:], op=ALU.mult)
            P.op('vector', 'tensor_tensor', [sgb[i], pyb], [sgb[i]], out=sgb[i][:], in0=pyb[:], in1=sgb[i][:], op=ALU.mult)
            P.op('gpsimd', 'tensor_add', [sga[i], sgb[i]], [mT], out=mT[:, c, :], in0=sga[i][:], in1=sgb[i][:])
        for tl in range(4):
            t = j * 4 + tl
            i = t % 2
            rows = slice(t * 128, (t + 1) * 128)
            P.dma('sync', xt[i][:], xh[HALF + t * 128:HALF + (t + 1) * 128, :])
            for n in range(2):
                ps = PS[n]
                for k in range(8):
                    P.op('tensor', 'matmul', [mT, wo], [ps], ps[:], lhsT=mT[:, k, tl * 128:(tl + 1) * 128],
                         rhs=wo[:, k, n * 512:(n + 1) * 512], start=(k == 0), stop=(k == 7))
                P.op('vector', 'scalar_tensor_tensor', [xt[i], ps], [yt[i]], out=yt[i][:, n * 512:(n + 1) * 512],
                     in0=xt[i][:, n * 512:(n + 1) * 512], scalar=ALPHA, in1=ps[:], op0=ALU.mult, op1=ALU.add)
            layer_norm(P, yt[i][:], ot[i][:], gam, bet, "mx", scr)
            P.dma('sync', x1[rows, :], ot[i][:])


def mixer_consts(h):
    j = np.arange(128)[:, None]
    q = np.arange(128)[None, :]
    cur = (j <= q).astype(np.float32)
    prev = (j >= q).astype(np.float32)
    m2 = np.concatenate([cur, prev], axis=1)
    mask2 = np.ascontiguousarray(np.stack([m2, m2], axis=1))
    hmask = np.ascontiguousarray(np.stack([prev, prev], axis=1)) * float(h)
    flag = np.full((128, 1), float(h), np.float32)
    return mask2, hmask.astype(np.float32), flag


def mixer_weights(w):
    chp = np.zeros((128, 8, 8), np.float32)
    def put(j, v):
        chp[:, :, j] = v.reshape(8, 128).T
    for i in range(4):
        put(i, w["conv_w"][i])
    put(4, w["conv_b"])
    put(5, w["b_rg"])
    put(6, w["b_ig"])
    put(7, w["lru_lambda"])
    return {"w_in": w["w_in"], "chp": chp, "w_rg": np.ascontiguousarray(w["w_rg"].reshape(D, 256)),
            "w_ig": np.ascontiguousarray(w["w_ig"].reshape(D, 256)), "w_rnn_out": w["w_rnn_out"],
            "w_att_out": w["w_att_out"], "w_out": w["w_out"], "ln1_g": w["ln1_g"], "ln1_b": w["ln1_b"]}


def run_mixer(x_full, w):
    nc = _get("mixer", build_mixer)
    mw = mixer_weights(w)
    in_maps = []
    zeros = np.zeros((HALF, D), np.float32)
    for c in range(8):
        b, h = c // 2, c % 2
        if h == 0:
            xh = np.concatenate([zeros, x_full[b, 0:HALF]], axis=0)
        else:
            xh = np.ascontiguousarray(x_full[b])
        mask2, hmask, flag = mixer_consts(h)
        m = dict(mw)
        m.update({"xh": xh, "mask2": mask2, "hmask": hmask, "flag": flag})
        in_maps.append(m)
    res = run_bass_kernel_spmd(nc, in_maps, core_ids=list(range(8)))
    out = np.empty((4, SEQ, D), np.float32)
    for c in range(8):
        b, h = c // 2, c % 2
        out[b, h * HALF:(h + 1) * HALF] = res.results[c]["x1"]
    return out


def build_moe(ne=NE):
    P = Prog()
    x1 = P.dram("x1", [HALF, D], F32, "ExternalInput")
    wr = P.dram("w_router", [D, ne], F32, "ExternalInput")
    br = P.dram("b_router", [ne], F32, "ExternalInput")
    wg = P.dram("w_gate", [ne, D, D], F32, "ExternalInput")
    wu = P.dram("w_up", [ne, D, D], F32, "ExternalInput")
    wd = P.dram("w_down", [ne, D, D], F32, "ExternalInput")
    bgp = P.dram("bgp", [128, ne, 8], F32, "ExternalInput")
    bup = P.dram("bup", [128, ne, 8], F32, "ExternalInput")
    bd = P.dram("b_down", [ne, D], F32, "ExternalInput")
    g2 = P.dram("ln2_g", [D], F32, "ExternalInput")
    b2 = P.dram("ln2_b", [D], F32, "ExternalInput")
    x2 = P.dram("x2", [HALF, D], F32, "ExternalOutput")
    emit_moe(P, ne, x1, wr, br, wg, wu, wd, bgp, bup, bd, g2, b2, x2)
    P.finish()
    return P.build()


def emit_moe(P, ne, x1, wr_d, br_d, wg_d, wu_d, wd_d, bgp_d, bup_d, bd_d, g2, b2, x2):
    ident = make_ident(P)
    PS = [P.psum("ps%d" % i, [128, 512], F32) for i in range(8)]
    xT = P.sbuf("m_xT", [128, 8, HALF], BF16)
    acc = P.sbuf("m_acc", [128, NT, D], F32)
    gd = P.sbuf("m_gd", [128, NT, ne], F32)
    bgp = P.sbuf("m_bgp", [128, ne, 8], F32)
    bup = P.sbuf("m_bup", [128, ne, 8], F32)
    P.dma('sync', bgp[:], bgp_d)
    P.dma('sync', bup[:], bup_d)
    gam = load_bcast(P, "m_gam", g2, D)
    bet = load_bcast(P, "m_bet", b2, D)
    brb = load_bcast(P, "m_brb", br_d, ne)
    P.make_arena(98 * 1024)
    wrf = P.sbuf("r_wrf", [128, 8, ne], F32, arena=True)
    wrh = P.sbuf("r_wrh", [128, 8, ne], BF16, arena=True)
    wrhf = P.sbuf("r_wrhf", [128, 8, ne], F32, arena=True)
    wrl = P.sbuf("r_wrl", [128, 8, ne], BF16, arena=True)
    P.dma('sync', wrf[:], wr_d.rearrange("(k p) n -> p k n", p=128))
    P.op('vector', 'tensor_copy', [wrf], [wrh], out=wrh[:], in_=wrf[:])
    P.op('vector', 'tensor_copy', [wrh], [wrhf], out=wrhf[:], in_=wrh[:])
    P.op('vector', 'tensor_tensor', [wrf, wrhf], [wrhf], out=wrhf[:], in0=wrf[:], in1=wrhf[:], op=ALU.subtract)
    P.op('vector', 'tensor_copy', [wrhf], [wrl], out=wrl[:], in_=wrhf[:])
    xa = [P.sbuf("r_x%d" % i, [128, D], F32, arena=True) for i in range(2)]
    xh_ = [P.sbuf("r_xh%d" % i, [128, D], BF16, arena=True) for i in range(2)]
    xlf = [P.sbuf("r_xlf%d" % i, [128, D], F32, arena=True) for i in range(2)]
    xl_ = [P.sbuf("r_xl%d" % i, [128, D], BF16, arena=True) for i in range(2)]
    xlT = [P.sbuf("r_xlT%d" % i, [128, 8, 128], BF16, arena=True) for i in range(2)]
    lg = [P.sbuf("r_lg%d" % i, [128, ne], F32, arena=True) for i in range(2)]
    ex = [P.sbuf("r_ex%d" % i, [128, ne], F32, arena=True) for i in range(2)]
    mk = [P.sbuf("r_mk%d" % i, [128, ne], F32, arena=True) for i in range(2)]
    m8 = [P.sbuf("r_m8%d" % i, [128, 8], F32, arena=True) for i in range(2)]
    sm = [P.sbuf("r_sm%d" % i, [128, 2], F32, arena=True) for i in range(2)]
    for t in range(NT):
        i = t % 2
        P.dma('sync', xa[i][:], x1[t * 128:(t + 1) * 128, :])
        P.op('vector', 'tensor_copy', [xa[i]], [xh_[i]], out=xh_[i][:], in_=xa[i][:])
        P.op('gpsimd', 'tensor_copy', [xh_[i]], [xlf[i]], out=xlf[i][:], in_=xh_[i][:])
        P.op('gpsimd', 'tensor_sub', [xa[i], xlf[i]], [xlf[i]], out=xlf[i][:], in0=xa[i][:], in1=xlf[i][:])
        P.op('gpsimd', 'tensor_copy', [xlf[i]], [xl_[i]], out=xl_[i][:], in_=xlf[i][:])
        ps = PS[t % 2]
        psb = ps[:].bitcast(BF16)
        for k in range(8):
            P.op('tensor', 'transpose', [xh_[i], ident], [ps], out=psb[:, k * 128:(k + 1) * 128],
                 in_=xh_[i][:, k * 128:(k + 1) * 128], identity=ident[:])
        P.op('scalar', 'copy', [ps], [xT], out=xT[:, :, t * 128:(t + 1) * 128],
             in_=psb.rearrange("p (k t) -> p k t", k=8))
        ps2 = PS[2 + t % 2]
        psb2 = ps2[:].bitcast(BF16)
        for k in range(8):
            P.op('tensor', 'transpose', [xl_[i], ident], [ps2], out=psb2[:, k * 128:(k + 1) * 128],
                 in_=xl_[i][:, k * 128:(k + 1) * 128], identity=ident[:])
        P.op('scalar', 'copy', [ps2], [xlT[i]], out=xlT[i][:], in_=psb2.rearrange("p (k t) -> p k t", k=8))
        pl = PS[4 + t % 2]
        nmm = 0
        for k in range(8):
            for (a_, asrc, b_) in ((xT[:, k, t * 128:(t + 1) * 128], xT, wrh), (xT[:, k, t * 128:(t + 1) * 128], xT, wrl),
                                   (xlT[i][:, k, :], xlT[i], wrh)):
                P.op('tensor', 'matmul', [asrc, b_], [pl], pl[:, 0:ne], lhsT=a_, rhs=b_[:, k, :],
                     start=(nmm == 0), stop=(nmm == 23))
                nmm += 1
        P.op('vector', 'tensor_tensor', [pl, brb], [lg[i]], out=lg[i][:], in0=pl[:, 0:ne], in1=brb[:], op=ALU.add)
        P.op('vector', 'max', [lg[i]], [m8[i]], out=m8[i][:], in_=lg[i][:])
        P.op('vector', 'tensor_scalar_mul', [m8[i]], [sm[i]], out=sm[i][:, 0:1], in0=m8[i][:, 0:1], scalar1=-1.0)
        P.op('scalar', 'activation', [lg[i], sm[i]], [ex[i]], out=ex[i][:], in_=lg[i][:], func=AF.Exp, bias=sm[i][:, 0:1])
        P.op('vector', 'tensor_scalar', [lg[i], m8[i]], [mk[i]], out=mk[i][:], in0=lg[i][:], scalar1=m8[i][:, 3:4],
             scalar2=None, op0=ALU.is_ge)
        P.op('vector', 'tensor_tensor', [ex[i], mk[i]], [ex[i]], out=ex[i][:], in0=ex[i][:], in1=mk[i][:], op=ALU.mult)
        P.op('vector', 'reduce_sum', [ex[i]], [sm[i]], out=sm[i][:, 1:2], in_=ex[i][:], axis=mybir.AxisListType.X)
        P.op('vector', 'reciprocal', [sm[i]], [sm[i]], out=sm[i][:, 1:2], in_=sm[i][:, 1:2])
        P.op('vector', 'tensor_scalar', [ex[i], sm[i]], [gd], out=gd[:, t, :], in0=ex[i][:], scalar1=sm[i][:, 1:2],
             scalar2=None, op0=ALU.mult)
    P.arena_reset(0)
    NSL = 4
    slab = [(P.sbuf("e_sl%d" % i, [128, 8, 128], BF16, arena=True), P.sbuf("e_ss%d" % i, [128, 8, 128], F32, arena=True))
            for i in range(NSL)]
    wdb = P.sbuf("e_wd", [128, 8, D], BF16, arena=True)
    wds = [P.sbuf("e_wds%d" % i, [128, D], F32, arena=True) for i in range(1)] * 2
    hT = P.sbuf("e_hT", [128, 8, HALF], BF16, arena=True)
    gs = [P.sbuf("e_g%d" % i, [128, 512], F32, arena=True) for i in range(2)]
    sg = [P.sbuf("e_sg%d" % i, [128, 512], F32, arena=True) for i in range(2)]
    u0 = [P.sbuf("e_u%d" % i, [128, 512], F32, arena=True) for i in range(2)]
    tmp = [P.sbuf("e_t%d" % i, [128, 512], F32, arena=True) for i in range(2)]
    bdb = [P.sbuf("e_bd%d" % i, [128, D], F32, arena=True) for i in range(1)] * 2
    sli = 0
    for e in range(ne):
        P.dma('sync', bdb[e % 2][:], bd_d[e].partition_broadcast(128))
        wgv = wg_d[e].rearrange("(k p) n -> p k n", p=128)
        wuv = wu_d[e].rearrange("(k p) n -> p k n", p=128)
        for fc in range(8):
            sl = []
            for wv in (wgv, wuv):
                t_, s_ = slab[sli % NSL]
                sli += 1
                P.dma('sync', s_[:], wv[:, :, fc * 128:(fc + 1) * 128])
                P.op('gpsimd', 'tensor_copy', [s_], [t_], out=t_[:], in_=s_[:])
                sl.append(t_)
            for j in range(4):
                i = (fc * 4 + j) % 2
                cs = slice(j * 512, (j + 1) * 512)
                pg, pu = PS[i], PS[2 + i]
                for k in range(8):
                    P.op('tensor', 'matmul', [sl[0], xT], [pg], pg[:], lhsT=sl[0][:, k, :], rhs=xT[:, k, cs],
                         start=(k == 0), stop=(k == 7))
                for k in range(8):
                    P.op('tensor', 'matmul', [sl[1], xT], [pu], pu[:], lhsT=sl[1][:, k, :], rhs=xT[:, k, cs],
                         start=(k == 0), stop=(k == 7))
                P.op('vector', 'tensor_scalar', [pg, bgp], [gs[i]], out=gs[i][:], in0=pg[:], scalar1=bgp[:, e, fc:fc + 1],
                     scalar2=7.0, op0=ALU.add, op1=ALU.min)
                P.op('scalar', 'activation', [gs[i]], [sg[i]], out=sg[i][:], in_=gs[i][:], func=AF.Sigmoid, scale=1.702)
                P.op('scalar', 'activation', [pu, bup], [u0[i]], out=u0[i][:], in_=pu[:], func=AF.Identity,
                     bias=bup[:, e, fc:fc + 1])
                P.op('gpsimd', 'tensor_scalar', [u0[i]], [u0[i]], out=u0[i][:], in0=u0[i][:], scalar1=7.0, scalar2=-7.0,
                     op0=ALU.min, op1=ALU.max)
                P.op('vector', 'tensor_tensor', [gs[i], sg[i]], [sg[i]], out=sg[i][:], in0=gs[i][:], in1=sg[i][:], op=ALU.mult)
                P.op('vector', 'scalar_tensor_tensor', [u0[i], sg[i]], [hT], out=hT[:, fc, cs], in0=u0[i][:], scalar=1.0,
                     in1=sg[i][:], op0=ALU.add, op1=ALU.mult)
        for k in range(8):
            P.dma('sync', wds[k % 2][:], wd_d[e, k * 128:(k + 1) * 128, :])
            P.op('gpsimd', 'tensor_copy', [wds[k % 2]], [wdb], out=wdb[:, k, :], in_=wds[k % 2][:])
        for t in range(NT):
            for n in range(2):
                i = (t * 2 + n) % 2
                pd = PS[4 + (t * 2 + n) % 4]
                for fc in range(8):
                    P.op('tensor', 'matmul', [hT, wdb], [pd], pd[:], lhsT=hT[:, fc, t * 128:(t + 1) * 128],
                         rhs=wdb[:, fc, n * 512:(n + 1) * 512], start=(fc == 0), stop=(fc == 7))
                P.op('vector', 'tensor_tensor', [pd, bdb[e % 2]], [tmp[i]], out=tmp[i][:], in0=pd[:],
                     in1=bdb[e % 2][:, n * 512:(n + 1) * 512], op=ALU.add)
                av = acc[:, t, n * 512:(n + 1) * 512]
                if e == 0:
                    P.op('vector', 'tensor_scalar', [tmp[i], gd], [acc], out=av, in0=tmp[i][:], scalar1=gd[:, t, e:e + 1],
                         scalar2=None, op0=ALU.mult)
                else:
                    P.op('vector', 'scalar_tensor_tensor', [tmp[i], gd, acc], [acc], out=av, in0=tmp[i][:],
                         scalar=gd[:, t, e:e + 1], in1=av, op0=ALU.mult, op1=ALU.add)
    P.arena_reset(0)
    xt = [P.sbuf("f_x%d" % i, [128, D], F32, arena=True) for i in range(2)]
    yt = [P.sbuf("f_y%d" % i, [128, D], F32, arena=True) for i in range(2)]
    ot = [P.sbuf("f_o%d" % i, [128, D], F32, arena=True) for i in range(2)]
    scr = ln_scratch(P, "moe")
    for t in range(NT):
        i = t % 2
        P.dma('sync', xt[i][:], x1[t * 128:(t + 1) * 128, :])
        P.op('vector', 'scalar_tensor_tensor', [xt[i], acc], [yt[i]], out=yt[i][:], in0=xt[i][:], scalar=ALPHA,
             in1=acc[:, t, :], op0=ALU.mult, op1=ALU.add)
        layer_norm(P, yt[i][:], ot[i][:], gam, bet, "moe", scr)
        P.dma('sync', x2[t * 128:(t + 1) * 128, :], ot[i][:])


def moe_weights(w, ne=NE):
    def lay(b):
        return np.ascontiguousarray(b.reshape(ne, 8, 128).transpose(2, 0, 1))
    return {"w_router": w["w_router"], "b_router": w["b_router"], "w_gate": w["w_gate"], "w_up": w["w_up"],
            "w_down": w["w_down"], "bgp": lay(w["b_gate"]), "bup": lay(w["b_up"]), "b_down": w["b_down"],
            "ln2_g": w["ln2_g"], "ln2_b": w["ln2_b"]}


def run_moe(x1_full, w):
    nc = _get("moe", build_moe)
    mw = moe_weights(w)
    in_maps = []
    for c in range(8):
        b, h = c // 2, c % 2
        m = dict(mw)
        m["x1"] = np.ascontiguousarray(x1_full[b, h * HALF:(h + 1) * HALF])
        in_maps.append(m)
    res = run_bass_kernel_spmd(nc, in_maps, core_ids=list(range(8)))
    out = np.empty((4, SEQ, D), np.float32)
    for c in range(8):
        b, h = c // 2, c % 2
        out[b, h * HALF:(h + 1) * HALF] = res.results[c]["x2"]
    return out


def kernel(**inputs):
    inp = {k: np.asarray(v) for k, v in inputs.items()}
    x = np.ascontiguousarray(inp["x"], dtype=np.float32)
    for l in range(2):
        w = {k: np.ascontiguousarray(v[l]) for k, v in inp.items() if k not in ("x", "p")}
        x = run_mixer(x, w)
        x = run_moe(x, w)
        x = run_ple(x, np.ascontiguousarray(inp["p"][l]), w)
    return x
```
